# Optimizing a Trainium2 kernel written in Bass

```python
import jax, jax.numpy as jnp
from jax import lax
import numpy as np

D_MODEL = 2048
BATCH = 2
SEQ = 4096
DEPTH = 1

N_Q_HEADS = 32
N_KV_HEADS = 8
HEAD_DIM = 64
Q_PER_KV = N_Q_HEADS // N_KV_HEADS
WINDOW = 128
ATTN_BLOCK = 128
ROT_DIM = HEAD_DIM // 4
ROPE_THETA = 500000.0
SSD_HEADS = 32
SSD_HEAD_DIM = 64
SSD_INNER = SSD_HEADS * SSD_HEAD_DIM
SSD_GROUPS = 8
SSD_STATE = 128
SSD_CONV = 4
SSD_CHUNK = 128
HEADS_PER_GROUP = SSD_HEADS // SSD_GROUPS
ATTN_WIDTH = N_Q_HEADS * HEAD_DIM
KV_WIDTH = N_KV_HEADS * HEAD_DIM
MIX_WIDTH = ATTN_WIDTH + SSD_INNER
BC_WIDTH = SSD_GROUPS * SSD_STATE
CONV_CH = SSD_INNER + 2 * BC_WIDTH
IN_PROJ_WIDTH = ATTN_WIDTH + 2 * KV_WIDTH + SSD_INNER + CONV_CH + SSD_HEADS
D_FF = 5632
FFN_CONV = 3
EPS = 1e-6

kernel_name = 'hymba_swa_sink_ssd_convffn'


def _rmsnorm(x, g):
    xf = x.astype(jnp.float32)
    y = xf * lax.rsqrt(jnp.mean(xf * xf, axis=-1, keepdims=True) + EPS)
    return (y * g.astype(jnp.float32)).astype(x.dtype)


def _causal_dwconv(x, w, b):
    k_taps = w.shape[0]
    s = x.shape[1]
    xp = jnp.pad(x, ((0, 0), (k_taps - 1, 0), (0, 0)))
    out = b
    for t in range(k_taps):
        out = out + xp[:, t:t + s] * w[t]
    return out


def _partial_rope(x, pos):
    half = ROT_DIM // 2
    inv = 1.0 / (ROPE_THETA ** (jnp.arange(0, ROT_DIM, 2, dtype=jnp.float32) / ROT_DIM))
    ang = pos[:, None] * inv[None, :]
    cos = jnp.cos(ang)[None, :, None, :]
    sin = jnp.sin(ang)[None, :, None, :]
    xf = x.astype(jnp.float32)
    x1 = xf[..., :half]
    x2 = xf[..., half:ROT_DIM]
    out = jnp.concatenate([x1 * cos - x2 * sin, x2 * cos + x1 * sin, xf[..., ROT_DIM:]], axis=-1)
    return out.astype(x.dtype)


def _band_blocks(t):
    b, s, h, d = t.shape
    nb = s // ATTN_BLOCK
    tp = jnp.pad(t, ((0, 0), (ATTN_BLOCK, 0), (0, 0), (0, 0)))
    prev = tp[:, :s].reshape(b, nb, ATTN_BLOCK, h, d)
    cur = t.reshape(b, nb, ATTN_BLOCK, h, d)
    return jnp.concatenate([prev, cur], axis=2)


def _sliding_window_attention(q, k, v, sinks):
    b, s, _, d = q.shape
    nb = s // ATTN_BLOCK
    qb = q.reshape(b, nb, ATTN_BLOCK, N_KV_HEADS, Q_PER_KV, d)
    kb = _band_blocks(k)
    vb = _band_blocks(v)
    scores = jnp.einsum('bnqhgd,bnkhd->bnhgqk', qb, kb, preferred_element_type=jnp.float32) * (d ** -0.5)
    qi = jnp.arange(ATTN_BLOCK)[:, None]
    kj = jnp.arange(2 * ATTN_BLOCK)[None, :]
    rel = qi + ATTN_BLOCK - kj
    band = (rel >= 0) & (rel < WINDOW)
    blk = jnp.arange(nb)[:, None, None]
    valid = band[None] & ((blk > 0) | (kj >= ATTN_BLOCK)[None])
    scores = jnp.where(valid[None, :, None, None], scores, -jnp.inf)
    sink = sinks.astype(jnp.float32).reshape(N_KV_HEADS, Q_PER_KV)[None, None, :, :, None, None]
    m = jnp.maximum(jnp.max(scores, axis=-1, keepdims=True), sink)
    p = jnp.exp(scores - m)
    probs = p / (jnp.sum(p, axis=-1, keepdims=True) + jnp.exp(sink - m))
    out = jnp.einsum('bnhgqk,bnkhd->bnqhgd', probs.astype(v.dtype), vb)
    return out.reshape(b, s, N_Q_HEADS * d)


def _ssd_chunked(xh, dt, a, bm, cm):
    b, s, _, p = xh.shape
    nc = s // SSD_CHUNK
    g, r, n = SSD_GROUPS, HEADS_PER_GROUP, SSD_STATE
    x_c = (xh * dt[..., None]).reshape(b, nc, SSD_CHUNK, g, r, p)
    a_cs = jnp.cumsum((dt * a).reshape(b, nc, SSD_CHUNK, g, r), axis=2)
    b_c = bm.reshape(b, nc, SSD_CHUNK, g, n)
    c_c = cm.reshape(b, nc, SSD_CHUNK, g, n)
    seg = a_cs[:, :, :, None] - a_cs[:, :, None, :]
    causal = jnp.tril(jnp.ones((SSD_CHUNK, SSD_CHUNK), dtype=bool))[None, None, :, :, None, None]
    l_mat = jnp.exp(jnp.where(causal, seg, -jnp.inf))
    cb = jnp.einsum('bclgn,bcsgn->bclsg', c_c, b_c)
    y_diag = jnp.einsum('bclsgr,bcsgrp->bclgrp', cb[..., None] * l_mat, x_c)
    decay_s = jnp.exp(a_cs[:, :, -1:] - a_cs)
    states = jnp.einsum('bcsgn,bcsgrp->bcgrpn', b_c, x_c * decay_s[..., None])
    chunk_decay = jnp.exp(a_cs[:, :, -1])

    def step(h, inp):
        s_c, d_c = inp
        return h * d_c[..., None, None] + s_c, h

    h0 = jnp.zeros((b, g, r, p, n), dtype=jnp.float32)
    _, prev = lax.scan(step, h0, (jnp.moveaxis(states, 1, 0), jnp.moveaxis(chunk_decay, 1, 0)))
    prev = jnp.moveaxis(prev, 0, 1)
    y_off = jnp.einsum('bclgn,bcgrpn->bclgrp', c_c, prev) * jnp.exp(a_cs)[..., None]
    return (y_diag + y_off).reshape(b, s, SSD_HEADS, p)


def _gated_group_rmsnorm(y, z, g):
    b, s, _ = y.shape
    yg = y * jax.nn.silu(z.astype(jnp.float32))
    yr = yg.reshape(b, s, SSD_GROUPS, -1)
    yr = yr * lax.rsqrt(jnp.mean(yr * yr, axis=-1, keepdims=True) + EPS)
    return yr.reshape(b, s, SSD_INNER) * g.astype(jnp.float32)


def _hybrid_mixer(xn, w_in, sinks, attn_out_norm, ssd_conv_w, ssd_conv_b, dt_bias, a_log, ssd_d, ssd_norm, w_out):
    b, s, _ = xn.shape
    proj = xn @ w_in
    o = np.cumsum([0, ATTN_WIDTH, KV_WIDTH, KV_WIDTH, SSD_INNER, CONV_CH, SSD_HEADS])
    q = proj[..., o[0]:o[1]].reshape(b, s, N_Q_HEADS, HEAD_DIM)
    k = proj[..., o[1]:o[2]].reshape(b, s, N_KV_HEADS, HEAD_DIM)
    v = proj[..., o[2]:o[3]].reshape(b, s, N_KV_HEADS, HEAD_DIM)
    z = proj[..., o[3]:o[4]]
    xbc = proj[..., o[4]:o[5]]
    dt_raw = proj[..., o[5]:o[6]]
    pos = jnp.arange(s, dtype=jnp.float32)
    q = _partial_rope(q, pos)
    k = _partial_rope(k, pos)
    attn = _rmsnorm(_sliding_window_attention(q, k, v, sinks), attn_out_norm)
    xbc = jax.nn.silu(_causal_dwconv(xbc, ssd_conv_w, ssd_conv_b)).astype(jnp.float32)
    xs = xbc[..., :SSD_INNER].reshape(b, s, SSD_HEADS, SSD_HEAD_DIM)
    bm = xbc[..., SSD_INNER:SSD_INNER + BC_WIDTH].reshape(b, s, SSD_GROUPS, SSD_STATE)
    cm = xbc[..., SSD_INNER + BC_WIDTH:].reshape(b, s, SSD_GROUPS, SSD_STATE)
    dt = jax.nn.softplus(dt_raw.astype(jnp.float32) + dt_bias.astype(jnp.float32))
    a = -jnp.exp(a_log.astype(jnp.float32))
    y = _ssd_chunked(xs, dt, a, bm, cm) + ssd_d.astype(jnp.float32)[:, None] * xs
    y = _gated_group_rmsnorm(y.reshape(b, s, SSD_INNER), z, ssd_norm).astype(xn.dtype)
    return jnp.concatenate([attn, y], axis=-1) @ w_out


def _conv_ffn(hn, w_up, ffn_conv_w, ffn_conv_b, w_down):
    u = _causal_dwconv(hn @ w_up, ffn_conv_w, ffn_conv_b)
    gate, val = u[..., :D_FF], u[..., D_FF:]
    return (jax.nn.silu(gate) * val) @ w_down


def setup_inputs(seed: int = 0) -> dict:
    key = jax.random.key(seed)
    ks = jax.random.split(key, 20)
    f32 = jnp.float32
    nrm = lambda k, shape, scale: jax.random.normal(k, shape, f32) * scale
    x = jax.random.normal(ks[0], (BATCH, SEQ, D_MODEL), f32)
    dt0 = jnp.exp(jax.random.uniform(ks[8], (DEPTH, SSD_HEADS), f32, np.log(1e-3), np.log(1e-1)))
    return {
        'x': x,
        'norm_mix': 1.0 + nrm(ks[1], (DEPTH, D_MODEL), 0.01),
        'w_in': nrm(ks[2], (DEPTH, D_MODEL, IN_PROJ_WIDTH), D_MODEL ** -0.5),
        'sinks': nrm(ks[3], (DEPTH, N_Q_HEADS), 0.5),
        'attn_out_norm': 1.0 + nrm(ks[4], (DEPTH, ATTN_WIDTH), 0.01),
        'ssd_conv_w': nrm(ks[5], (DEPTH, SSD_CONV, CONV_CH), SSD_CONV ** -0.5),
        'ssd_conv_b': nrm(ks[6], (DEPTH, CONV_CH), 0.01),
        'dt_bias': dt0 + jnp.log(-jnp.expm1(-dt0)),
        'a_log': jnp.log(jax.random.uniform(ks[9], (DEPTH, SSD_HEADS), f32, 1.0, 16.0)),
        'ssd_d': 1.0 + nrm(ks[10], (DEPTH, SSD_HEADS), 0.01),
        'ssd_norm': 1.0 + nrm(ks[11], (DEPTH, SSD_INNER), 0.01),
        'w_out': nrm(ks[12], (DEPTH, MIX_WIDTH, D_MODEL), MIX_WIDTH ** -0.5),
        'norm_ffn': 1.0 + nrm(ks[13], (DEPTH, D_MODEL), 0.01),
        'w_up': nrm(ks[14], (DEPTH, D_MODEL, 2 * D_FF), D_MODEL ** -0.5),
        'ffn_conv_w': nrm(ks[15], (DEPTH, FFN_CONV, 2 * D_FF), FFN_CONV ** -0.5),
        'ffn_conv_b': nrm(ks[16], (DEPTH, 2 * D_FF), 0.01),
        'w_down': nrm(ks[17], (DEPTH, D_FF, D_MODEL), D_FF ** -0.5),
        'norm_final': 1.0 + nrm(ks[18], (D_MODEL,), 0.01),
    }


def reference(x, norm_mix, w_in, sinks, attn_out_norm, ssd_conv_w, ssd_conv_b, dt_bias, a_log, ssd_d, ssd_norm, w_out, norm_ffn, w_up, ffn_conv_w, ffn_conv_b, w_down, norm_final):
    h = x
    for l in range(DEPTH):
        h = h + _hybrid_mixer(_rmsnorm(h, norm_mix[l]), w_in[l], sinks[l], attn_out_norm[l], ssd_conv_w[l], ssd_conv_b[l], dt_bias[l], a_log[l], ssd_d[l], ssd_norm[l], w_out[l])
        h = h + _conv_ffn(_rmsnorm(h, norm_ffn[l]), w_up[l], ffn_conv_w[l], ffn_conv_b[l], w_down[l])
    return _rmsnorm(h, norm_final)
```

```python
import numpy as np
import ml_dtypes
from contextlib import ExitStack
import concourse.bass as bass
import concourse.mybir as mybir
from concourse.bass_utils import run_bass_kernel_spmd

F32 = mybir.dt.float32
BF16 = mybir.dt.bfloat16
ALU = mybir.AluOpType
AF = mybir.ActivationFunctionType

NCORES = 8
D = 2048
DFF = 5632
FFC = DFF // NCORES
EPS = 1e-6
ENGS = ("pe", "act", "dve", "pool", "sp")
SAME_ENGINE_SYNC = True
PHASED = True
import os as _os
DBG_SKIPCC = bool(int(_os.environ.get('KSKIPCC', '0')))


class Buf:
    def __init__(self, t, name):
        self.t = t
        self.name = name
        self.w = {}
        self.r = {}
        self.dsem = None
        self.dcnt = 0
        self.excl = False

    def __getitem__(self, k):
        return self.t[k]


class Prog:
    def __init__(self, nc, es):
        self.nc = nc
        self.es = es
        self.es_sem = es
        self.active = True
        self.q = {e: [] for e in ENGS}
        self.cnt = {e: 0 for e in ENGS}
        self.waited = {e: {} for e in ENGS}
        self.semh = {}
        self.nsem = 0
        self.dbufs = []
        for e in ("pe", "act", "dve", "pool"):
            self.semh[("e", e)] = self._newsem("c_" + e)

    def _newsem(self, name):
        self.nsem += 1
        return self.es_sem.enter_context(self.nc.semaphore(name + "_%d" % self.nsem))

    def sb(self, name, shape, dt):
        return Buf(self.es.enter_context(self.nc.sbuf_tensor("s_" + name, list(shape), dt)), name)

    def _deps(self, eng, reads, writes):
        deps = {}
        for b in reads:
            for k, v in b.w.items():
                deps[k] = max(deps.get(k, 0), v)
            if b.excl:
                for k, v in b.r.items():
                    if k != ("e", eng):
                        deps[k] = max(deps.get(k, 0), v)
        for b in writes:
            for k, v in b.w.items():
                deps[k] = max(deps.get(k, 0), v)
            for k, v in b.r.items():
                deps[k] = max(deps.get(k, 0), v)
        waits = []
        for k, v in deps.items():
            if k == ("e", eng) and (eng == "pe" or not SAME_ENGINE_SYNC):
                continue
            if self.waited[eng].get(k, 0) < v:
                waits.append((k, v))
                self.waited[eng][k] = v
        return waits

    def _record(self, ev, reads, writes):
        k, v = ev
        for b in reads:
            b.r[k] = max(b.r.get(k, 0), v)
        for b in writes:
            b.w = {k: v}
            b.r = {}

    def op(self, eng, fn, reads=(), writes=()):
        if not self.active:
            return
        waits = self._deps(eng, reads, writes)
        self.cnt[eng] += 1
        ev = (("e", eng), self.cnt[eng])
        self._record(ev, reads, writes)
        sem = self.semh[("e", eng)]
        semh = self.semh

        def emit(e):
            for k, v in waits:
                e.wait_ge(semh[k], v)
            fn(e).then_inc(sem, 1)
        self.q[eng].append(emit)

    def dma(self, queue, out, in_, reads=(), writes=(), sem_buf=None):
        if not self.active:
            return
        waits = self._deps(queue, reads, writes)
        if sem_buf.dsem is None:
            sem_buf.dsem = ("d", id(sem_buf))
            self.semh[sem_buf.dsem] = self._newsem("d_" + sem_buf.name)
            self.dbufs.append(sem_buf)
        sem_buf.dcnt += 1
        ev = (sem_buf.dsem, 16 * sem_buf.dcnt)
        self._record(ev, reads, writes)
        sem = self.semh[sem_buf.dsem]
        semh = self.semh

        def emit(e):
            for k, v in waits:
                e.wait_ge(semh[k], v)
            e.dma_start(out=out, in_=in_).then_inc(sem, 16)
        self.q[queue].append(emit)

    def collective(self, in_buf, out_buf):
        if PHASED:
            return
        waits = self._deps("pool", [in_buf], [out_buf])
        if getattr(self, "last_coll", None) is not None and self.waited["pool"].get(self.last_coll, 0) < 1:
            waits.append((self.last_coll, 1))
            self.waited["pool"][self.last_coll] = 1
        key = ("c", id(out_buf))
        self.last_coll = key
        self.semh[key] = self._newsem("cc_" + out_buf.name)
        self._record((key, 1), [in_buf], [out_buf])
        sem = self.semh[key]
        semh = self.semh

        def emit(e):
            for k, v in waits:
                e.wait_ge(semh[k], v)
            if DBG_SKIPCC:
                e.memset(self.dbg_t[:, :], 0.0).then_inc(sem, 1)
                return
            e.collective_compute("AllGather", ALU.bypass, replica_groups=[list(range(NCORES))],
                                 ins=[in_buf.t.ap().opt()], outs=[out_buf.t.ap().opt()]).then_inc(sem)
            e.wait_ge(sem, 1)
        self.q["pool"].append(emit)
        self.waited["pool"][key] = 1

    def barrier(self):
        if not self.active:
            return
        evs = [(("e", e), self.cnt[e]) for e in ("pe", "act", "dve", "pool") if self.cnt[e] > 0]
        evs += [(b.dsem, 16 * b.dcnt) for b in self.dbufs]
        semh = self.semh
        for eng in ENGS:
            waits = []
            for k, v in evs:
                if k == ("e", eng):
                    continue
                if self.waited[eng].get(k, 0) < v:
                    waits.append((k, v))
                    self.waited[eng][k] = v

            def emit(e, waits=waits):
                for k, v in waits:
                    e.wait_ge(semh[k], v)
            self.q[eng].append(emit)

    def final_wait(self, eng, bufs):
        if not self.active:
            return
        semh = self.semh
        waits = [(b.dsem, 16 * b.dcnt) for b in bufs]

        def emit(e):
            for k, v in waits:
                e.wait_ge(semh[k], v)
        self.q[eng].append(emit)

    def emit_all(self):
        with self.nc.Block() as block:
            @block.tensor
            def _(e):
                for f in self.q["pe"]:
                    f(e)

            @block.scalar
            def _(e):
                for f in self.q["act"]:
                    f(e)

            @block.vector
            def _(e):
                for f in self.q["dve"]:
                    f(e)

            @block.gpsimd
            def _(e):
                for f in self.q["pool"]:
                    f(e)

            @block.sync
            def _(e):
                for f in self.q["sp"]:
                    f(e)
        self.q = {e: [] for e in ENGS}


def build_nc(B, S, phase=0):
    NT = B * S
    NTILE = NT // 128
    TPB = S // 128
    NG = NT // 512
    GPB = S // 512
    nc = bass.Bass("TRN2", target_bir_lowering=False)

    def din(name, shape, dt=F32):
        return nc.dram_tensor(name, list(shape), dt, kind="ExternalInput").ap()

    x_ap = din("x", [NT, D])
    xT_ap = din("xT", [256, NT])
    wtm_ap = din("w_tm", [D, 644])
    wfm_ap = din("w_fm", [D, 512])
    wout_ap = din("w_out", [4096, 256])
    wup_ap = din("w_up", [D, 1536])
    wdn_ap = din("w_down", [DFF, 256])
    gmix_ap = din("gmix", [128, 16])
    gout_ap = din("gout", [128, 32])
    gffn_ap = din("gffn", [128, 16])
    gfin_ap = din("gfin", [128, 2])
    hp_ap = din("hparams", [128, 16])
    scw_ap = din("ssd_cw", [128, 16])
    scb_ap = din("ssd_cb", [128, 4])
    fcw_ap = din("ffn_cw", [128, 36])
    fcb_ap = din("ffn_cb", [128, 12])
    cos_ap = din("cosr", [128, TPB * 40])
    sin_ap = din("sinr", [128, TPB * 40])
    identb_ap = din("identb", [128, 128], BF16)
    ident32_ap = din("ident32", [128, 128])
    triu_ap = din("triu32", [128, 128])
    negm_ap = din("negm4", [128, 512])
    mcur_ap = din("mcur", [128, 512], BF16)
    mprev_ap = din("mprev", [128, 512], BF16)
    out_ap = nc.dram_tensor("out", [256, NT], F32, kind=("ExternalOutput" if (not PHASED or phase == 5) else "Internal")).ap()

    def dbuf(name, shape, dt, prod, cons):
        if PHASED and phase == prod:
            kind = "ExternalOutput"
        elif PHASED and phase in cons:
            kind = "ExternalInput"
        else:
            kind = "Internal"
        return Buf(nc.dram_tensor(name, list(shape), dt, kind=kind), name)

    ag1_in = dbuf("ag1_in", [512, NT], BF16, 1, ())
    ag1_out = dbuf("ag1_out", [4096, NT], BF16, -1, (2,))
    ag1s_in = dbuf("ag1s_in", [NTILE, 128], F32, 1, ())
    ag1s_out = dbuf("ag1s_out", [8 * NTILE, 128], F32, -1, (2,))
    ag2_in = dbuf("ag2_in", [256, NT], BF16, 2, ())
    ag2_out = dbuf("ag2_out", [2048, NT], BF16, -1, (3,))
    ag2s_in = dbuf("ag2s_in", [1, NT], F32, 2, ())
    ag2s_out = dbuf("ag2s_out", [8, NT], F32, -1, (3,))
    ag3_in = dbuf("ag3_in", [FFC, NT], BF16, 3, ())
    ag3_out = dbuf("ag3_out", [DFF, NT], BF16, -1, (4,))
    ag4_in = dbuf("ag4_in", [1, NT], F32, 4, ())
    ag4_out = dbuf("ag4_out", [8, NT], F32, -1, (5,))
    hT_imp = dbuf("hT_imp", [256, NT], F32, -1, (4, 5))
    hT_exp = dbuf("hT_exp", [256, NT], F32, phase if phase in (2, 4) else -2, ())

    with ExitStack() as es:
        P = Prog(nc, es)
        pb = [Buf(es.enter_context(nc.psum_tensor("pb%d" % k, [128, 512], F32)), "pb%d" % k) for k in range(8)]

        for b_ in pb:
            b_.excl = True

        def pbf(k):
            return pb[k].t[:, :].bitcast(BF16)

        identb = P.sb("identb", [128, 128], BF16)
        ident32 = P.sb("ident32", [128, 128], F32)
        triu = P.sb("triu", [128, 128], F32)
        ones32 = P.sb("ones32", [128, 128], F32)
        gfin = P.sb("gfin", [128, 2], F32)
        hT = P.sb("hT", [128, 2, NT], F32)
        for b_, a_ in ((identb, identb_ap), (ident32, ident32_ap), (triu, triu_ap), (gfin, gfin_ap)):
            P.dma("sp", b_[:, :], a_, writes=[b_], sem_buf=b_)
        P.op("dve", lambda e: e.memset(ones32[:, :], 1.0), writes=[ones32])
        epsb = P.sb("epsb", [128, 1], F32)
        P.dbg_t = P.sb("dbg_t", [128, 4], F32)
        P.op("dve", lambda e: e.memset(epsb[:, :], EPS), writes=[epsb])

        stage = [P.sb("stage%d" % k, [128, 1536], F32) for k in range(3)]
        stage_i = [0]

        def load_weight(dst, src_ap, nk, ncol, gain, eng_cycle=("dve", "pool")):
            for kc in range(nk):
                st = stage[stage_i[0] % 3]
                stage_i[0] += 1
                P.dma("sp", st[:, 0:ncol], src_ap[kc * 128:(kc + 1) * 128, :], writes=[st], sem_buf=st)
                eng = eng_cycle[kc % len(eng_cycle)]
                if gain is None:
                    P.op(eng, lambda e, st=st, kc=kc: e.tensor_copy(out=dst[:, kc, 0:ncol], in_=st[:, 0:ncol]),
                         reads=[st], writes=[dst])
                else:
                    P.op(eng, lambda e, st=st, kc=kc: e.tensor_scalar(
                        out=dst[:, kc, 0:ncol], in0=st[:, 0:ncol], scalar1=gain[:, kc:kc + 1], scalar2=None,
                        op0=ALU.mult), reads=[st, gain], writes=[dst])

        with ExitStack() as es1:
            P.es = es1
            P.active = (not PHASED) or phase == 1
            wtm = P.sb("wtm", [128, 16, 644], BF16)
            wfm = P.sb("wfm", [128, 16, 512], BF16)
            gmix = P.sb("gmix", [128, 16], F32)
            hp = P.sb("hp", [128, 16], F32)
            scw = P.sb("scw", [128, 16], F32)
            scb = P.sb("scb", [128, 4], F32)
            cosr = P.sb("cosr", [128, TPB * 40], F32)
            sinr = P.sb("sinr", [128, TPB * 40], F32)
            negm = P.sb("negm", [128, 512], F32)
            mcur = P.sb("mcur", [128, 512], BF16)
            mprev = P.sb("mprev", [128, 512], BF16)
            for b_, a_ in ((gmix, gmix_ap), (hp, hp_ap), (scw, scw_ap), (scb, scb_ap), (cosr, cos_ap),
                           (sinr, sin_ap), (negm, negm_ap), (mcur, mcur_ap), (mprev, mprev_ap)):
                P.dma("sp", b_[:, :], a_, writes=[b_], sem_buf=b_)
            load_weight(wtm, wtm_ap, 16, 644, gmix, ("dve",))
            load_weight(wfm, wfm_ap, 16, 512, gmix, ("pool",))

            esink = P.sb("esink", [128, 4], F32)
            abc = P.sb("abc", [128, 4], F32)
            P.op("act", lambda e: e.activation(out=esink[:, :], in_=hp[:, 0:4], func=AF.Exp), reads=[hp], writes=[esink])
            P.op("act", lambda e: e.activation(out=abc[:, :], in_=hp[:, 8:12], func=AF.Exp), reads=[hp], writes=[abc])
            P.op("dve", lambda e: e.tensor_scalar(out=abc[:, :], in0=abc[:, :], scalar1=-1.0, scalar2=None, op0=ALU.mult),
                 reads=[abc], writes=[abc])

            xt = [P.sb("xt%d" % k, [128, D], F32) for k in range(3)]
            junk = P.sb("junk", [128, D], BF16)
            xb = P.sb("xb", [128, D], BF16)
            xnTa = [P.sb("xnTa%d" % k, [128, 8, 128], BF16) for k in range(2)]
            xnTb = [P.sb("xnTb%d" % k, [128, 8, 128], BF16) for k in range(2)]
            ss = P.sb("ss", [128, 1], F32)
            rstd = P.sb("rstd", [128, 1], F32)
            qkb = P.sb("qkb", [128, 5, 64], BF16)
            rt = [P.sb("rt%d" % k, [128, 40], F32) for k in range(4)]
            qT = P.sb("qT", [64, 512], BF16)
            kT = [P.sb("kT%d" % k, [64, 128], BF16) for k in range(2)]
            vb = [P.sb("vb%d" % k, [128, 65], BF16) for k in range(2)]
            pT = P.sb("pT", [128, 2, 512], BF16)
            den = P.sb("den", [128, 4], F32)
            attn32 = P.sb("attn32", [128, 256], F32)
            junk2 = P.sb("junk2", [128, 256], F32)
            ss_all = P.sb("ss_all", [128, NTILE], F32)
            mixb = P.sb("mixb", [128, 512], BF16)
            mixT = [P.sb("mixT%d" % k, [128, 4, 128], BF16) for k in range(2)]
            xraw = [P.sb("xraw%d" % k, [128, 4, 131], F32) for k in range(2)]
            cacc = P.sb("cacc", [128, 4, 128], F32)
            xc32 = P.sb("xc32", [128, 4, 128], F32)
            bcb = P.sb("bcb", [128, 2, 128], BF16)
            dtt = P.sb("dtt", [128, 4], F32)
            dta = P.sb("dta", [128, 4], F32)
            dte = P.sb("dte", [128, 4], F32)
            dt = P.sb("dt", [128, 4], F32)
            da = P.sb("da", [128, 4], F32)
            daB = P.sb("daB", [128, 512], F32)
            acs = P.sb("acs", [128, 4], F32)
            nacs = P.sb("nacs", [128, 4], F32)
            ein = P.sb("ein", [128, 12], F32)
            eout = P.sb("eout", [128, 12], F32)
            LT = P.sb("LT", [128, 4, 128], BF16)
            MT = P.sb("MT", [128, 4, 128], BF16)
            xs32 = P.sb("xs32", [128, 256], F32)
            btok = P.sb("btok", [128, 128], BF16)
            xcb = P.sb("xcb", [128, 256], BF16)
            xd = P.sb("xd", [128, 256], BF16)
            prev32 = P.sb("prev32", [128, 256], F32)
            prevb = P.sb("prevb", [128, 256], BF16)
            ysb = P.sb("ysb", [128, 256], F32)
            y32 = P.sb("y32", [128, 256], F32)
            sz = P.sb("sz", [128, 256], F32)
            yg = P.sb("yg", [128, 256], F32)
            ssy = P.sb("ssy", [128, 1], F32)
            rsy = P.sb("rsy", [128, 1], F32)
            sst = P.sb("sst", [NTILE, 128], F32)

            for k in range(2):
                P.op("dve", lambda e, k=k: e.memset(vb[k][:, 64:65], 1.0), writes=[vb[k]])
                P.op("dve", lambda e, k=k: e.memset(xraw[k][:, :, :], 0.0), writes=[xraw[k]])

            def load_x(i):
                if i < NTILE:
                    P.dma("sp", xt[i % 3][:, :], x_ap[i * 128:(i + 1) * 128, :], writes=[xt[i % 3]], sem_buf=xt[i % 3])

            load_x(0)
            load_x(1)
            for i in range(NTILE):
                par = i % 2
                first = (i % TPB == 0)
                j = i % TPB
                xti = xt[i % 3]
                xa, xbn = xnTa[par], xnTb[par]
                load_x(i + 2)
                P.op("act", lambda e, xti=xti: e.activation(out=junk[:, :], in_=xti[:, :], func=AF.Square, accum_out=ss[:, 0:1]),
                     reads=[xti], writes=[junk, ss])
                P.op("act", lambda e: e.activation(out=rstd[:, :], in_=ss[:, :], func=AF.Ln, scale=1.0 / D, bias=epsb[:, 0:1]),
                     reads=[ss, epsb], writes=[rstd])
                P.op("act", lambda e: e.activation(out=rstd[:, :], in_=rstd[:, :], func=AF.Exp, scale=-0.5), reads=[rstd], writes=[rstd])
                P.op("pool", lambda e, xti=xti: e.tensor_scalar(out=xb[:, :], in0=xti[:, :], scalar1=rstd[:, 0:1], scalar2=None,
                                                                op0=ALU.mult), reads=[xti, rstd], writes=[xb])

                def f_tp(e):
                    for kc in range(16):
                        ins = e.transpose(out=pbf(kc // 8)[:, (kc % 8) * 128:(kc % 8 + 1) * 128],
                                          in_=xb[:, kc * 128:(kc + 1) * 128], identity=identb[:, :])
                    return ins
                P.op("pe", f_tp, reads=[xb, identb], writes=[pb[0], pb[1]])
                P.op("dve", lambda e, xa=xa: e.tensor_copy(out=xa[:, :, :], in_=pbf(0).rearrange("p (k t) -> p k t", k=8)),
                     reads=[pb[0]], writes=[xa])
                P.op("act", lambda e, xbn=xbn: e.activation(out=xbn[:, :, :], in_=pbf(1).rearrange("p (k t) -> p k t", k=8), func=AF.Copy),
                     reads=[pb[1]], writes=[xbn])

                def xn(kc, xa=xa, xbn=xbn):
                    return (xa if kc < 8 else xbn)[:, kc % 8, :]

                def f_tm(e, xn=xn):
                    for kc in range(16):
                        e.matmul(pb[2][:, 0:388], lhsT=xn(kc), rhs=wtm[:, kc, 0:388], start=(kc == 0), stop=(kc == 15))
                        ins = e.matmul(pb[3][:, 0:256], lhsT=xn(kc), rhs=wtm[:, kc, 388:644], start=(kc == 0), stop=(kc == 15))
                    return ins
                P.op("pe", f_tm, reads=[xa, xbn, wtm], writes=[pb[2], pb[3]])

                def f_fm(e, xn=xn):
                    for c in range(4):
                        for kc in range(16):
                            ins = e.matmul(pb[4][:, c * 128:(c + 1) * 128], lhsT=wfm[:, kc, c * 128:(c + 1) * 128], rhs=xn(kc),
                                           start=(kc == 0), stop=(kc == 15))
                    return ins
                P.op("pe", f_fm, reads=[xa, xbn, wfm], writes=[pb[4]])

                qk3 = pb[2][:, 0:320].rearrange("p (h d) -> p h d", h=5)
                cosv = cosr[:, j * 40:(j + 1) * 40].rearrange("p (h d) -> p h d", h=5)
                sinv = sinr[:, j * 40:(j + 1) * 40].rearrange("p (h d) -> p h d", h=5)
                r3 = [t_[:, :].rearrange("p (h d) -> p h d", h=5) for t_ in rt]
                P.op("dve", lambda e, qk3=qk3: e.tensor_copy(out=qkb[:, :, :], in_=qk3), reads=[pb[2]], writes=[qkb])
                P.op("dve", lambda e, qk3=qk3, cosv=cosv: e.tensor_tensor(out=r3[0], in0=qk3[:, :, 0:8], in1=cosv, op=ALU.mult),
                     reads=[pb[2], cosr], writes=[rt[0]])
                P.op("dve", lambda e, qk3=qk3, sinv=sinv: e.tensor_tensor(out=r3[1], in0=qk3[:, :, 8:16], in1=sinv, op=ALU.mult),
                     reads=[pb[2], sinr], writes=[rt[1]])
                P.op("dve", lambda e, qk3=qk3, cosv=cosv: e.tensor_tensor(out=r3[2], in0=qk3[:, :, 8:16], in1=cosv, op=ALU.mult),
                     reads=[pb[2], cosr], writes=[rt[2]])
                P.op("dve", lambda e, qk3=qk3, sinv=sinv: e.tensor_tensor(out=r3[3], in0=qk3[:, :, 0:8], in1=sinv, op=ALU.mult),
                     reads=[pb[2], sinr], writes=[rt[3]])
                P.op("dve", lambda e: e.tensor_tensor(out=qkb[:, :, 0:8], in0=r3[0], in1=r3[1], op=ALU.subtract),
                     reads=[rt[0], rt[1]], writes=[qkb])
                P.op("dve", lambda e: e.tensor_tensor(out=qkb[:, :, 8:16], in0=r3[2], in1=r3[3], op=ALU.add),
                     reads=[rt[2], rt[3]], writes=[qkb])
                vcur, vprev = vb[par], vb[1 - par]
                kcur, kprev = kT[par], kT[1 - par]
                P.op("act", lambda e, vcur=vcur: e.activation(out=vcur[:, 0:64], in_=pb[2][:, 320:384], func=AF.Copy),
                     reads=[pb[2]], writes=[vcur])

                def f_qkT(e):
                    for h in range(5):
                        ins = e.transpose(out=pbf(5)[0:64, h * 128:(h + 1) * 128], in_=qkb[:, h, :], identity=identb[:, :])
                    return ins
                P.op("pe", f_qkT, reads=[qkb, identb], writes=[pb[5]])
                P.op("dve", lambda e: e.tensor_copy(out=qT[:, :], in_=pbf(5)[0:64, 0:512]), reads=[pb[5]], writes=[qT])
                P.op("act", lambda e, kcur=kcur: e.activation(out=kcur[:, :], in_=pbf(5)[0:64, 512:640], func=AF.Copy),
                     reads=[pb[5]], writes=[kcur])

                if not first:
                    P.op("pe", lambda e, kprev=kprev: e.matmul(pb[6][:, :], lhsT=kprev[:, :], rhs=qT[:, :], start=True, stop=True),
                         reads=[kprev, qT], writes=[pb[6]])
                    P.op("act", lambda e: e.activation(out=pT[:, 0, :], in_=pb[6][:, :], func=AF.Exp, scale=0.125),
                         reads=[pb[6]], writes=[pT])
                P.op("pe", lambda e, kcur=kcur: e.matmul(pb[7][:, :], lhsT=kcur[:, :], rhs=qT[:, :], start=True, stop=True),
                     reads=[kcur, qT], writes=[pb[7]])
                P.op("act", lambda e: e.activation(out=pT[:, 1, :], in_=pb[7][:, :], func=AF.Exp, scale=0.125),
                     reads=[pb[7]], writes=[pT])
                if not first:
                    P.op("pool", lambda e: e.tensor_tensor(out=pT[:, 0, :], in0=pT[:, 0, :], in1=mprev[:, :], op=ALU.mult),
                         reads=[pT, mprev], writes=[pT])
                P.op("pool", lambda e: e.tensor_tensor(out=pT[:, 1, :], in0=pT[:, 1, :], in1=mcur[:, :], op=ALU.mult),
                     reads=[pT, mcur], writes=[pT])
                po = pb[6][:, :].rearrange("p (h d) -> p h d", h=4)

                def f_pv(e, first=first, vcur=vcur, vprev=vprev, po=po):
                    for h in range(4):
                        if not first:
                            e.matmul(po[:, h, 0:65], lhsT=pT[:, 0, h * 128:(h + 1) * 128], rhs=vprev[:, :], start=True, stop=False)
                        ins = e.matmul(po[:, h, 0:65], lhsT=pT[:, 1, h * 128:(h + 1) * 128], rhs=vcur[:, :], start=first, stop=True)
                    return ins
                P.op("pe", f_pv, reads=[pT, vcur, vprev], writes=[pb[6]])
                P.op("dve", lambda e, po=po: e.tensor_tensor(out=den[:, :], in0=po[:, :, 64], in1=esink[:, :], op=ALU.add),
                     reads=[pb[6], esink], writes=[den])
                P.op("dve", lambda e: e.reciprocal(out=den[:, :], in_=den[:, :]), reads=[den], writes=[den])
                for h in range(4):
                    P.op("dve", lambda e, h=h, po=po: e.tensor_scalar(out=attn32[:, h * 64:(h + 1) * 64], in0=po[:, h, 0:64],
                                                                      scalar1=den[:, h:h + 1], scalar2=None, op0=ALU.mult),
                         reads=[pb[6], den], writes=[attn32])
                P.op("dve", lambda e, i=i: e.scalar_tensor_tensor(out=junk2[:, :], in0=attn32[:, :], scalar=1.0, in1=attn32[:, :],
                                                                   op0=ALU.mult, op1=ALU.mult, accum_out=ss_all[:, i:i + 1]),
                     reads=[attn32], writes=[junk2, ss_all])
                P.op("pool", lambda e: e.tensor_copy(out=mixb[:, 0:256], in_=attn32[:, :]), reads=[attn32], writes=[mixb])

                xr, xrp = xraw[par], xraw[1 - par]
                P.op("act", lambda e, xr=xr: e.activation(out=xr[:, :, 3:131], in_=pb[4][:, :].rearrange("p (c t) -> p c t", c=4),
                                                          func=AF.Copy), reads=[pb[4]], writes=[xr])
                if first:
                    P.op("pool", lambda e, xr=xr: e.memset(xr[:, :, 0:3], 0.0), writes=[xr])
                else:
                    P.op("pool", lambda e, xr=xr, xrp=xrp: e.tensor_copy(out=xr[:, :, 0:3], in_=xrp[:, :, 128:131]),
                         reads=[xrp], writes=[xr])
                for c in range(4):
                    P.op("dve", lambda e, c=c, xr=xr: e.tensor_scalar(out=cacc[:, c, :], in0=xr[:, c, 3:131],
                                                                      scalar1=scw[:, c * 4 + 3:c * 4 + 4], scalar2=scb[:, c:c + 1],
                                                                      op0=ALU.mult, op1=ALU.add), reads=[xr, scw, scb], writes=[cacc])
                    for k in range(3):
                        P.op("dve", lambda e, c=c, k=k, xr=xr: e.scalar_tensor_tensor(
                            out=cacc[:, c, :], in0=xr[:, c, k:k + 128], scalar=scw[:, c * 4 + k:c * 4 + k + 1], in1=cacc[:, c, :],
                            op0=ALU.mult, op1=ALU.add), reads=[xr, scw, cacc], writes=[cacc])
                P.op("act", lambda e: e.activation(out=xc32[:, :, :], in_=cacc[:, :, :], func=AF.Silu), reads=[cacc], writes=[xc32])
                P.op("pool", lambda e: e.tensor_copy(out=bcb[:, :, :], in_=xc32[:, 2:4, :]), reads=[xc32], writes=[bcb])

                P.op("dve", lambda e: e.tensor_tensor(out=dtt[:, :], in0=pb[2][:, 384:388], in1=hp[:, 4:8], op=ALU.add),
                     reads=[pb[2], hp], writes=[dtt])
                P.op("dve", lambda e: e.tensor_scalar(out=dta[:, :], in0=dtt[:, :], scalar1=-1.0, scalar2=None, op0=ALU.mult),
                     reads=[dtt], writes=[dta])
                P.op("dve", lambda e: e.tensor_tensor(out=dta[:, :], in0=dta[:, :], in1=dtt[:, :], op=ALU.max),
                     reads=[dtt, dta], writes=[dta])
                P.op("act", lambda e: e.activation(out=dte[:, :], in_=dta[:, :], func=AF.Exp, scale=-1.0), reads=[dta], writes=[dte])
                P.op("act", lambda e: e.activation(out=dte[:, :], in_=dte[:, :], func=AF.Ln, bias=1.0), reads=[dte], writes=[dte])
                P.op("dve", lambda e: e.scalar_tensor_tensor(out=dt[:, :], in0=dtt[:, :], scalar=0.0, in1=dte[:, :],
                                                             op0=ALU.max, op1=ALU.add), reads=[dtt, dte], writes=[dt])
                P.op("dve", lambda e: e.tensor_tensor(out=da[:, :], in0=dt[:, :], in1=abc[:, :], op=ALU.mult),
                     reads=[dt, abc], writes=[da])
                for h in range(4):
                    P.op("pool", lambda e, h=h: e.tensor_scalar(out=daB[:, h * 128:(h + 1) * 128], in0=triu[:, :],
                                                                scalar1=da[:, h:h + 1], scalar2=None, op0=ALU.mult),
                         reads=[triu, da], writes=[daB])

                def f_acs(e):
                    e.matmul(pb[0][:, :], lhsT=ones32[:, :], rhs=daB[:, :], start=True, stop=False)
                    e.matmul(pb[0][:, :], lhsT=ident32[:, :], rhs=negm[:, :], start=False, stop=True)
                    return e.matmul(pb[1][:, 0:4], lhsT=triu[:, :], rhs=da[:, :], start=True, stop=True)
                P.op("pe", f_acs, reads=[ones32, daB, ident32, negm, triu, da], writes=[pb[0], pb[1]])
                pB3 = pb[0][:, :].rearrange("p (h l) -> p h l", h=4)
                P.op("dve", lambda e: e.tensor_copy(out=acs[:, :], in_=pb[1][:, 0:4]), reads=[pb[1]], writes=[acs])
                P.op("dve", lambda e: e.tensor_scalar(out=nacs[:, :], in0=acs[:, :], scalar1=-1.0, scalar2=None, op0=ALU.mult),
                     reads=[acs], writes=[nacs])
                P.op("dve", lambda e, pB3=pB3: e.tensor_copy(out=ein[:, 0:4], in_=pB3[:, :, 127]), reads=[pb[0]], writes=[ein])
                P.op("dve", lambda e: e.tensor_tensor(out=ein[:, 4:8], in0=ein[:, 0:4], in1=acs[:, :], op=ALU.subtract),
                     reads=[ein, acs], writes=[ein])
                P.op("dve", lambda e: e.tensor_copy(out=ein[:, 8:12], in_=acs[:, :]), reads=[acs], writes=[ein])
                P.op("act", lambda e: e.activation(out=eout[:, :], in_=ein[:, :], func=AF.Exp), reads=[ein], writes=[eout])
                for h in range(4):
                    P.op("act", lambda e, h=h, pB3=pB3: e.activation(out=LT[:, h, :], in_=pB3[:, h, :], func=AF.Exp,
                                                                     bias=nacs[:, h:h + 1]), reads=[pb[0], nacs], writes=[LT])
                P.op("pe", lambda e: e.matmul(pb[3][:, 256:384], lhsT=bcb[:, 0, :], rhs=bcb[:, 1, :], start=True, stop=True),
                     reads=[bcb], writes=[pb[3]])
                for h in range(4):
                    P.op("dve", lambda e, h=h: e.tensor_tensor(out=MT[:, h, :], in0=LT[:, h, :], in1=pb[3][:, 256:384], op=ALU.mult),
                         reads=[LT, pb[3]], writes=[MT])

                def f_tr2(e):
                    e.transpose(out=pb[5][:, 0:128], in_=xc32[:, 0, :], identity=ident32[:, :])
                    e.transpose(out=pb[5][:, 128:256], in_=xc32[:, 1, :], identity=ident32[:, :])
                    return e.transpose(out=pbf(5)[:, 512:640], in_=bcb[:, 0, :], identity=identb[:, :])
                P.op("pe", f_tr2, reads=[xc32, bcb, ident32, identb], writes=[pb[5]])
                P.op("act", lambda e: e.activation(out=xs32[:, :], in_=pb[5][:, 0:256], func=AF.Copy), reads=[pb[5]], writes=[xs32])
                P.op("dve", lambda e: e.tensor_copy(out=btok[:, :], in_=pbf(5)[:, 512:640]), reads=[pb[5]], writes=[btok])
                for h in range(4):
                    P.op("dve", lambda e, h=h: e.tensor_scalar(out=xcb[:, h * 64:(h + 1) * 64], in0=xs32[:, h * 64:(h + 1) * 64],
                                                               scalar1=dt[:, h:h + 1], scalar2=None, op0=ALU.mult),
                         reads=[xs32, dt], writes=[xcb])
                    P.op("dve", lambda e, h=h: e.tensor_scalar(out=xd[:, h * 64:(h + 1) * 64], in0=xs32[:, h * 64:(h + 1) * 64],
                                                               scalar1=dt[:, h:h + 1], scalar2=eout[:, 4 + h:5 + h],
                                                               op0=ALU.mult, op1=ALU.mult), reads=[xs32, dt, eout], writes=[xd])
                if first:
                    P.op("dve", lambda e: e.memset(prev32[:, :], 0.0), writes=[prev32])
                    P.op("pool", lambda e: e.memset(prevb[:, :], 0.0), writes=[prevb])

                def f_y(e):
                    for h in range(4):
                        e.matmul(pb[7][:, h * 64:(h + 1) * 64], lhsT=MT[:, h, :], rhs=xcb[:, h * 64:(h + 1) * 64], start=True, stop=True)
                    e.matmul(pb[7][:, 256:512], lhsT=bcb[:, 1, :], rhs=prevb[:, :], start=True, stop=True)
                    return e.matmul(pb[4][:, 0:256], lhsT=btok[:, :], rhs=xd[:, :], start=True, stop=True)
                P.op("pe", f_y, reads=[MT, xcb, bcb, prevb, btok, xd], writes=[pb[7], pb[4]])
                for h in range(4):
                    sl = slice(h * 64, (h + 1) * 64)
                    P.op("dve", lambda e, h=h, sl=sl: e.scalar_tensor_tensor(out=ysb[:, sl], in0=xs32[:, sl], scalar=hp[:, 12 + h:13 + h],
                                                                             in1=pb[7][:, sl], op0=ALU.mult, op1=ALU.add),
                         reads=[xs32, hp, pb[7]], writes=[ysb])
                    P.op("dve", lambda e, h=h, sl=sl: e.scalar_tensor_tensor(out=y32[:, sl], in0=pb[7][:, 256 + h * 64:256 + (h + 1) * 64],
                                                                             scalar=eout[:, 8 + h:9 + h], in1=ysb[:, sl],
                                                                             op0=ALU.mult, op1=ALU.add),
                         reads=[pb[7], eout, ysb], writes=[y32])
                    P.op("dve", lambda e, h=h, sl=sl: e.scalar_tensor_tensor(out=prev32[:, sl], in0=prev32[:, sl], scalar=eout[:, h:h + 1],
                                                                             in1=pb[4][:, sl], op0=ALU.mult, op1=ALU.add),
                         reads=[prev32, eout, pb[4]], writes=[prev32])
                P.op("pool", lambda e: e.tensor_copy(out=prevb[:, :], in_=prev32[:, :]), reads=[prev32], writes=[prevb])
                P.op("act", lambda e: e.activation(out=sz[:, :], in_=pb[3][:, 0:256], func=AF.Silu), reads=[pb[3]], writes=[sz])
                P.op("dve", lambda e: e.tensor_tensor(out=yg[:, :], in0=y32[:, :], in1=sz[:, :], op=ALU.mult), reads=[y32, sz], writes=[yg])
                P.op("dve", lambda e: e.scalar_tensor_tensor(out=junk2[:, :], in0=yg[:, :], scalar=1.0, in1=yg[:, :],
                                                             op0=ALU.mult, op1=ALU.mult, accum_out=ssy[:, 0:1]),
                     reads=[yg], writes=[junk2, ssy])
                P.op("act", lambda e: e.activation(out=rsy[:, :], in_=ssy[:, :], func=AF.Ln, scale=1.0 / 256, bias=epsb[:, 0:1]),
                     reads=[ssy, epsb], writes=[rsy])
                P.op("act", lambda e: e.activation(out=rsy[:, :], in_=rsy[:, :], func=AF.Exp, scale=-0.5), reads=[rsy], writes=[rsy])
                P.op("dve", lambda e: e.tensor_scalar(out=mixb[:, 256:512], in0=yg[:, :], scalar1=rsy[:, 0:1], scalar2=None,
                                                      op0=ALU.mult), reads=[yg, rsy], writes=[mixb])

                def f_mT(e):
                    for f in range(4):
                        ins = e.transpose(out=pbf(5)[:, f * 128:(f + 1) * 128], in_=mixb[:, f * 128:(f + 1) * 128], identity=identb[:, :])
                    return ins
                P.op("pe", f_mT, reads=[mixb, identb], writes=[pb[5]])
                mt = mixT[par]
                P.op("act", lambda e, mt=mt: e.activation(out=mt[:, :, :], in_=pbf(5)[:, 0:512].rearrange("p (f t) -> p f t", f=4),
                                                          func=AF.Copy), reads=[pb[5]], writes=[mt])
                P.dma("sp", ag1_in.t.ap().rearrange("(f p) t -> p f t", p=128)[:, :, i * 128:(i + 1) * 128], mt[:, :, :],
                      reads=[mt], writes=[ag1_in], sem_buf=mt)

            P.op("pe", lambda e: e.transpose(out=pb[5][0:NTILE, 0:128], in_=ss_all[:, :], identity=ident32[:, :]),
                 reads=[ss_all, ident32], writes=[pb[5]])
            P.op("dve", lambda e: e.tensor_copy(out=sst[:, :], in_=pb[5][0:NTILE, 0:128]), reads=[pb[5]], writes=[sst])
            P.dma("sp", ag1s_in.t.ap(), sst[:, :], reads=[sst], writes=[ag1s_in], sem_buf=sst)
            P.collective(ag1_in, ag1_out)
            P.collective(ag1s_in, ag1s_out)
            P.barrier()
        P.es = es

        def rbc_from(ps_bank, ssg, dst, n):
            P.op("pe", lambda e: e.matmul(ps_bank[:, :], lhsT=ones32[0:8, :], rhs=ssg[:, :], start=True, stop=True),
                 reads=[ones32, ssg], writes=[ps_bank])
            P.op("act", lambda e: e.activation(out=dst[:, :], in_=ps_bank[:, :], func=AF.Ln, scale=1.0 / n, bias=epsb[:, 0:1]),
                 reads=[ps_bank, epsb], writes=[dst])
            P.op("act", lambda e: e.activation(out=dst[:, :], in_=dst[:, :], func=AF.Exp, scale=-0.5), reads=[dst], writes=[dst])

        def ss_row(src_views, sq, ps_bank, row, dram_buf, g):
            for n_, (v, rd) in enumerate(src_views):
                P.op("act", lambda e, v=v: e.activation(out=sq[:, :], in_=v, func=AF.Square), reads=rd, writes=[sq])
                P.op("pe", lambda e, n_=n_: e.matmul(ps_bank[0:1, :], lhsT=ones32[:, 0:1], rhs=sq[:, :],
                                                      start=(n_ == 0), stop=(n_ == len(src_views) - 1)),
                     reads=[ones32, sq], writes=[ps_bank])
            P.op("dve", lambda e: e.tensor_copy(out=row[:, :], in_=ps_bank[0:1, :]), reads=[ps_bank], writes=[row])
            P.dma("sp", dram_buf.t.ap()[:, g * 512:(g + 1) * 512], row[:, :], reads=[row], writes=[dram_buf], sem_buf=row)

        with ExitStack() as es2:
            P.es = es2
            P.active = (not PHASED) or phase == 2
            wo = P.sb("wo", [128, 32, 256], BF16)
            gout = P.sb("gout", [128, 32], F32)
            P.dma("sp", gout[:, :], gout_ap, writes=[gout], sem_buf=gout)
            load_weight(wo, wout_ap, 32, 256, gout)
            mixg = [P.sb("mixg%d" % k, [128, 32, 512], BF16) for k in range(2)]
            ssg = [P.sb("ssg%d" % k, [8, 512], F32) for k in range(2)]
            xTg = [P.sb("xTg%d" % k, [128, 2, 512], F32) for k in range(2)]
            rbc = P.sb("rbc", [128, 512], F32)
            t1 = P.sb("t1", [128, 512], F32)
            sq = P.sb("sq", [128, 512], F32)
            row = P.sb("row", [1, 512], F32)
            hbt = [P.sb("hbt%d" % k, [128, 2, 512], BF16) for k in range(2)]
            ag1v = ag1_out.t.ap().rearrange("(k p) t -> p k t", p=128)
            ag1sv = ag1s_out.t.ap().rearrange("(r i) p -> r (i p)", r=8)
            xTv = xT_ap.rearrange("(j p) t -> p j t", p=128)

            def load_g2(g):
                if g < NG:
                    k = g % 2
                    sl = slice(g * 512, (g + 1) * 512)
                    for q4 in range(4):
                        P.dma("sp", mixg[k][:, q4 * 8:(q4 + 1) * 8, :], ag1v[:, q4 * 8:(q4 + 1) * 8, sl], reads=[ag1_out],
                              writes=[mixg[k]], sem_buf=mixg[k])
                    P.dma("sp", ssg[k][:, :], ag1sv[:, sl], reads=[ag1s_out], writes=[ssg[k]], sem_buf=ssg[k])
                    P.dma("sp", xTg[k][:, :, :], xTv[:, :, sl], writes=[xTg[k]], sem_buf=xTg[k])
            load_g2(0)
            for g in range(NG):
                k = g % 2
                load_g2(g + 1)
                rbc_from(pb[0], ssg[k], rbc, 2048.0)
                hb = hbt[k]
                for jj in range(2):
                    pa, py = pb[1 + 2 * jj], pb[2 + 2 * jj]

                    def f_o(e, jj=jj, k=k, pa=pa, py=py):
                        ka = [r * 4 + f for r in range(8) for f in range(2)]
                        ky = [r * 4 + 2 + f for r in range(8) for f in range(2)]
                        for n_, kc in enumerate(ka):
                            e.matmul(pa[:, :], lhsT=wo[:, kc, jj * 128:(jj + 1) * 128], rhs=mixg[k][:, kc, :], start=(n_ == 0), stop=(n_ == 15))
                        for n_, kc in enumerate(ky):
                            ins = e.matmul(py[:, :], lhsT=wo[:, kc, jj * 128:(jj + 1) * 128], rhs=mixg[k][:, kc, :], start=(n_ == 0), stop=(n_ == 15))
                        return ins
                    P.op("pe", f_o, reads=[wo, mixg[k]], writes=[pa, py])
                    hsl = hT[:, jj, g * 512:(g + 1) * 512]
                    P.op("dve", lambda e, pa=pa: e.tensor_tensor(out=t1[:, :], in0=pa[:, :], in1=rbc[:, :], op=ALU.mult),
                         reads=[pa, rbc], writes=[t1])
                    P.op("dve", lambda e, py=py: e.tensor_tensor(out=t1[:, :], in0=t1[:, :], in1=py[:, :], op=ALU.add),
                         reads=[t1, py], writes=[t1])
                    P.op("dve", lambda e, hsl=hsl, jj=jj, k=k: e.tensor_tensor(out=hsl, in0=t1[:, :], in1=xTg[k][:, jj, :], op=ALU.add),
                         reads=[t1, xTg[k]], writes=[hT])
                    P.op("pool", lambda e, hsl=hsl, jj=jj, hb=hb: e.tensor_copy(out=hb[:, jj, :], in_=hsl), reads=[hT], writes=[hb])
                ss_row([(hT[:, 0, g * 512:(g + 1) * 512], [hT]), (hT[:, 1, g * 512:(g + 1) * 512], [hT])], sq, pb[5], row, ag2s_in, g)
                P.dma("sp", ag2_in.t.ap().rearrange("(j p) t -> p j t", p=128)[:, :, g * 512:(g + 1) * 512], hb[:, :, :],
                      reads=[hb], writes=[ag2_in], sem_buf=hb)
            if PHASED:
                P.dma("sp", hT_exp.t.ap().rearrange("(j p) t -> p j t", p=128), hT[:, :, :], reads=[hT], writes=[hT_exp], sem_buf=hT)
            P.collective(ag2_in, ag2_out)
            P.collective(ag2s_in, ag2s_out)
            P.barrier()
        P.es = es

        with ExitStack() as es3:
            P.es = es3
            P.active = (not PHASED) or phase == 3
            wu = P.sb("wu", [128, 16, 1536], BF16)
            gffn = P.sb("gffn", [128, 16], F32)
            fcw = P.sb("fcw", [128, 36], F32)
            fcb = P.sb("fcb", [128, 12], F32)
            for b_, a_ in ((gffn, gffn_ap), (fcw, fcw_ap), (fcb, fcb_ap)):
                P.dma("sp", b_[:, :], a_, writes=[b_], sem_buf=b_)
            load_weight(wu, wup_ap, 16, 1536, gffn)
            hg = [P.sb("hg%d" % k, [128, 16, 512], BF16) for k in range(2)]
            ssg3 = [P.sb("ssg3%d" % k, [8, 512], F32) for k in range(2)]
            rbc3 = P.sb("rbc3", [128, 512], F32)
            uraw = [P.sb("uraw%d" % m, [128, 514], F32) for m in range(12)]
            acc = [P.sb("acc%d" % k, [128, 512], F32) for k in range(2)]
            sg = P.sb("sg", [128, 512], F32)
            aT = [P.sb("aT%d" % k, [128, 512], BF16) for k in range(3)]
            ag2v = ag2_out.t.ap().rearrange("(k p) t -> p k t", p=128)
            for m in range(12):
                P.op("pool", lambda e, m=m: e.memset(uraw[m][:, :], 0.0), writes=[uraw[m]])

            def load_g3(g):
                if g < NG:
                    k = g % 2
                    sl = slice(g * 512, (g + 1) * 512)
                    for q4 in range(2):
                        P.dma("sp", hg[k][:, q4 * 8:(q4 + 1) * 8, :], ag2v[:, q4 * 8:(q4 + 1) * 8, sl], reads=[ag2_out],
                              writes=[hg[k]], sem_buf=hg[k])
                    P.dma("sp", ssg3[k][:, :], ag2s_out.t.ap()[:, sl], reads=[ag2s_out], writes=[ssg3[k]], sem_buf=ssg3[k])
            load_g3(0)
            na = 0
            for g in range(NG):
                k = g % 2
                load_g3(g + 1)
                rbc_from(pb[0], ssg3[k], rbc3, 2048.0)
                gfirst = (g % GPB == 0)
                for m in range(6):
                    for half in range(2):
                        mm = m + 6 * half
                        ps_ = pb[1 + ((m * 2 + half) % 6)]
                        ur = uraw[mm]

                        def f_u(e, mm=mm, k=k, ps_=ps_):
                            for kc in range(16):
                                ins = e.matmul(ps_[:, :], lhsT=wu[:, kc, mm * 128:(mm + 1) * 128], rhs=hg[k][:, kc, :],
                                               start=(kc == 0), stop=(kc == 15))
                            return ins
                        P.op("pe", f_u, reads=[wu, hg[k]], writes=[ps_])
                        if gfirst:
                            P.op("pool", lambda e, ur=ur: e.memset(ur[:, 0:2], 0.0), writes=[ur])
                        else:
                            P.op("pool", lambda e, ur=ur: e.tensor_copy(out=ur[:, 0:2], in_=ur[:, 512:514]), reads=[ur], writes=[ur])
                        P.op("dve", lambda e, ur=ur, ps_=ps_: e.tensor_tensor(out=ur[:, 2:514], in0=ps_[:, :], in1=rbc3[:, :], op=ALU.mult),
                             reads=[ps_, rbc3, ur], writes=[ur])
                        ac = acc[half]
                        P.op("dve", lambda e, ur=ur, ac=ac, mm=mm: e.tensor_scalar(out=ac[:, :], in0=ur[:, 2:514],
                                                                                   scalar1=fcw[:, mm * 3 + 2:mm * 3 + 3], scalar2=fcb[:, mm:mm + 1],
                                                                                   op0=ALU.mult, op1=ALU.add), reads=[ur, fcw, fcb], writes=[ac])
                        for kk in range(2):
                            P.op("dve", lambda e, ur=ur, ac=ac, mm=mm, kk=kk: e.scalar_tensor_tensor(
                                out=ac[:, :], in0=ur[:, kk:kk + 512], scalar=fcw[:, mm * 3 + kk:mm * 3 + kk + 1], in1=ac[:, :],
                                op0=ALU.mult, op1=ALU.add), reads=[ur, fcw, ac], writes=[ac])
                    P.op("act", lambda e: e.activation(out=sg[:, :], in_=acc[0][:, :], func=AF.Silu), reads=[acc[0]], writes=[sg])
                    at = aT[na % 3]
                    na += 1
                    P.op("pool", lambda e, at=at: e.tensor_tensor(out=at[:, :], in0=sg[:, :], in1=acc[1][:, :], op=ALU.mult),
                         reads=[sg, acc[1]], writes=[at])
                    rows = 128 if m < 5 else FFC - 5 * 128
                    P.dma("sp", ag3_in.t.ap()[m * 128:m * 128 + rows, g * 512:(g + 1) * 512], at[0:rows, :],
                          reads=[at], writes=[ag3_in], sem_buf=at)
            P.collective(ag3_in, ag3_out)
            P.barrier()
        P.es = es

        with ExitStack() as es4:
            P.es = es4
            P.active = (not PHASED) or phase == 4
            hTv = hT[:, :, :]
            if PHASED:
                P.dma("sp", hTv, hT_imp.t.ap().rearrange("(j p) t -> p j t", p=128), writes=[hT], sem_buf=hT)
            wd = P.sb("wd", [128, 44, 256], BF16)
            load_weight(wd, wdn_ap, 44, 256, None)
            agq = [P.sb("agq%d" % k, [128, 11, 512], BF16) for k in range(6)]
            sq4 = P.sb("sq4", [128, 512], F32)
            row4 = P.sb("row4", [1, 512], F32)
            ssg4 = [P.sb("ssg4%d" % k, [8, 512], F32) for k in range(2)]
            rbc4 = P.sb("rbc4", [128, 512], F32)
            ot = [P.sb("ot%d" % k, [128, 2, 512], F32) for k in range(2)]
            ag3v = ag3_out.t.ap().rearrange("(k p) t -> p k t", p=128)

            def load_q4(n):
                g, q4 = n // 4, n % 4
                if g < NG:
                    b_ = agq[n % 6]
                    P.dma("sp", b_[:, :, :], ag3v[:, q4 * 11:(q4 + 1) * 11, g * 512:(g + 1) * 512], reads=[ag3_out],
                          writes=[b_], sem_buf=b_)
            for n in range(4):
                load_q4(n)
            for g in range(NG):
                k = g % 2
                pss = [pb[1 + 2 * (g % 2)], pb[2 + 2 * (g % 2)]]
                for q4 in range(4):
                    n = g * 4 + q4
                    b_ = agq[n % 6]

                    def f_d(e, q4=q4, b_=b_, pss=pss):
                        for jj in range(2):
                            for kk in range(11):
                                kc = q4 * 11 + kk
                                ins = e.matmul(pss[jj][:, :], lhsT=wd[:, kc, jj * 128:(jj + 1) * 128], rhs=b_[:, kk, :],
                                               start=(kc == 0), stop=(kc == 43))
                        return ins
                    P.op("pe", f_d, reads=[wd, b_], writes=pss)
                    load_q4(n + 4)
                for jj in range(2):
                    ps_ = pss[jj]
                    hsl = hT[:, jj, g * 512:(g + 1) * 512]
                    P.op("dve", lambda e, hsl=hsl, ps_=ps_: e.tensor_tensor(out=hsl, in0=hsl, in1=ps_[:, :], op=ALU.add),
                         reads=[hT, ps_], writes=[hT])
                ss_row([(hT[:, 0, g * 512:(g + 1) * 512], [hT]), (hT[:, 1, g * 512:(g + 1) * 512], [hT])], sq4, pb[5], row4, ag4_in, g)
            if PHASED:
                P.dma("sp", hT_exp.t.ap().rearrange("(j p) t -> p j t", p=128), hT[:, :, :], reads=[hT], writes=[hT_exp], sem_buf=hT)
                P.barrier()
                P.active = (phase == 5)
                P.dma("sp", hTv, hT_imp.t.ap().rearrange("(j p) t -> p j t", p=128), writes=[hT], sem_buf=hT)
            P.collective(ag4_in, ag4_out)
            for g in range(NG):
                k = g % 2
                sl = slice(g * 512, (g + 1) * 512)
                P.dma("sp", ssg4[k][:, :], ag4_out.t.ap()[:, sl], reads=[ag4_out], writes=[ssg4[k]], sem_buf=ssg4[k])
                rbc_from(pb[0], ssg4[k], rbc4, 2048.0)
                o = ot[k]
                for jj in range(2):
                    P.op("dve", lambda e, jj=jj, o=o, sl=sl: e.scalar_tensor_tensor(out=o[:, jj, :], in0=hT[:, jj, sl], scalar=gfin[:, jj:jj + 1],
                                                                                    in1=rbc4[:, :], op0=ALU.mult, op1=ALU.mult),
                         reads=[hT, gfin, rbc4], writes=[o])
                P.dma("sp", out_ap.rearrange("(j p) t -> p j t", p=128)[:, :, sl], o[:, :, :], reads=[o], writes=[], sem_buf=o)
            P.final_wait("sp", ot)
            P.barrier()
        P.es = es
        P.emit_all()
    return nc


def _prep_inputs(inp, B, S):
    f32 = np.float32
    NT = B * S
    TPB = S // 128
    x = np.ascontiguousarray(np.asarray(inp["x"], dtype=f32).reshape(NT, D))
    w_in = np.asarray(inp["w_in"], dtype=f32)[0]
    w_out = np.asarray(inp["w_out"], dtype=f32)[0]
    w_up = np.asarray(inp["w_up"], dtype=f32)[0]
    w_down = np.asarray(inp["w_down"], dtype=f32)[0]
    o = np.cumsum([0, 2048, 512, 512, 2048, 4096, 32])

    def kc_layout(v):
        return np.ascontiguousarray(v.reshape(-1, 128).T.astype(f32))

    identb = np.eye(128, dtype=f32).astype(ml_dtypes.bfloat16)
    ident32 = np.eye(128, dtype=f32)
    triu = np.triu(np.ones((128, 128), dtype=f32))
    negm = np.where(np.arange(128)[None, :] < np.arange(128)[:, None], -30000.0, 0.0).astype(f32)
    negm4 = np.ascontiguousarray(np.tile(negm, (1, 4)))
    mcur = np.tile(triu, (1, 4)).astype(ml_dtypes.bfloat16)
    mprev = np.tile(1.0 - triu, (1, 4)).astype(ml_dtypes.bfloat16)
    pos = np.arange(S, dtype=f32)
    inv = (1.0 / (np.float32(500000.0) ** (np.arange(0, 16, 2, dtype=f32) / np.float32(16)))).astype(f32)
    ang = (pos[:, None] * inv[None, :]).astype(f32).astype(np.float64)
    cos = np.cos(ang).astype(f32).reshape(TPB, 128, 8)
    sin = np.sin(ang).astype(f32).reshape(TPB, 128, 8)
    cosr = np.ascontiguousarray(np.tile(cos[:, :, None, :], (1, 1, 5, 1)).transpose(1, 0, 2, 3).reshape(128, TPB * 40))
    sinr = np.ascontiguousarray(np.tile(sin[:, :, None, :], (1, 1, 5, 1)).transpose(1, 0, 2, 3).reshape(128, TPB * 40))
    norm_mix = np.asarray(inp["norm_mix"], f32)[0]
    attn_g = np.asarray(inp["attn_out_norm"], f32)[0]
    ssd_g = np.asarray(inp["ssd_norm"], f32)[0]
    norm_ffn = np.asarray(inp["norm_ffn"], f32)[0]
    norm_final = np.asarray(inp["norm_final"], f32)
    sinks = np.asarray(inp["sinks"], f32)[0]
    dt_bias = np.asarray(inp["dt_bias"], f32)[0]
    a_log = np.asarray(inp["a_log"], f32)[0]
    ssd_d = np.asarray(inp["ssd_d"], f32)[0]
    scw_full = np.asarray(inp["ssd_conv_w"], f32)[0]
    scb_full = np.asarray(inp["ssd_conv_b"], f32)[0]
    fcw_full = np.asarray(inp["ffn_conv_w"], f32)[0]
    fcb_full = np.asarray(inp["ffn_conv_b"], f32)[0]
    perm = np.concatenate([np.concatenate([np.arange(256 * r, 256 * r + 256), 2048 + np.arange(256 * r, 256 * r + 256)]) for r in range(8)])
    gout_full = np.concatenate([attn_g, ssd_g])[perm]
    maps = []
    for c in range(NCORES):
        q = w_in[:, o[0] + 256 * c:o[0] + 256 * (c + 1)]
        k = w_in[:, o[1] + 64 * c:o[1] + 64 * (c + 1)]
        v = w_in[:, o[2] + 64 * c:o[2] + 64 * (c + 1)]
        z = w_in[:, o[3] + 256 * c:o[3] + 256 * (c + 1)]
        xs_ = w_in[:, o[4] + 256 * c:o[4] + 256 * (c + 1)]
        bm = w_in[:, o[4] + 2048 + 128 * c:o[4] + 2048 + 128 * (c + 1)]
        cm = w_in[:, o[4] + 3072 + 128 * c:o[4] + 3072 + 128 * (c + 1)]
        dtc = w_in[:, o[5] + 4 * c:o[5] + 4 * (c + 1)]
        w_tm = np.ascontiguousarray(np.concatenate([q, k, v, dtc, z], axis=1))
        w_fm = np.ascontiguousarray(np.concatenate([xs_, bm, cm], axis=1))
        ch = np.concatenate([256 * c + np.arange(256), 2048 + 128 * c + np.arange(128), 3072 + 128 * c + np.arange(128)])
        scw = scw_full[:, ch].reshape(4, 4, 128).transpose(2, 1, 0).reshape(128, 16)
        scb = scb_full[ch].reshape(4, 128).T
        wup_c = np.zeros((D, 1536), f32)
        fcw = np.zeros((3, 1536), f32)
        fcb = np.zeros((1536,), f32)
        for half in range(2):
            src = slice(half * DFF + FFC * c, half * DFF + FFC * (c + 1))
            wup_c[:, half * 768:half * 768 + FFC] = w_up[:, src]
            fcw[:, half * 768:half * 768 + FFC] = fcw_full[:, src]
            fcb[half * 768:half * 768 + FFC] = fcb_full[src]
        hpar = np.concatenate([sinks[4 * c:4 * c + 4], dt_bias[4 * c:4 * c + 4], a_log[4 * c:4 * c + 4], ssd_d[4 * c:4 * c + 4]])
        maps.append({
            "x": x,
            "xT": np.ascontiguousarray(x[:, 256 * c:256 * (c + 1)].T),
            "w_tm": w_tm, "w_fm": w_fm,
            "w_out": np.ascontiguousarray(w_out[perm][:, 256 * c:256 * (c + 1)]),
            "w_up": wup_c,
            "w_down": np.ascontiguousarray(w_down[:, 256 * c:256 * (c + 1)]),
            "gmix": kc_layout(norm_mix), "gout": kc_layout(gout_full), "gffn": kc_layout(norm_ffn),
            "gfin": kc_layout(norm_final[256 * c:256 * (c + 1)]),
            "hparams": np.ascontiguousarray(np.tile(hpar[None, :], (128, 1)).astype(f32)),
            "ssd_cw": np.ascontiguousarray(scw.astype(f32)), "ssd_cb": np.ascontiguousarray(scb.astype(f32)),
            "ffn_cw": np.ascontiguousarray(fcw.reshape(3, 12, 128).transpose(2, 1, 0).reshape(128, 36)),
            "ffn_cb": np.ascontiguousarray(fcb.reshape(12, 128).T),
            "cosr": cosr, "sinr": sinr, "identb": identb, "ident32": ident32, "triu32": triu, "negm4": negm4,
            "mcur": mcur, "mprev": mprev,
        })
    return maps


_NC_CACHE = {}


def _run(B, S, phase, maps):
    key = (B, S, phase)
    if key not in _NC_CACHE:
        _NC_CACHE[key] = build_nc(B, S, phase)
    return run_bass_kernel_spmd(_NC_CACHE[key], maps, core_ids=list(range(NCORES))).results


def kernel(**inputs):
    x = np.asarray(inputs["x"])
    B, S, _ = x.shape
    maps = _prep_inputs(inputs, B, S)
    cat = lambda rs, k: np.ascontiguousarray(np.concatenate([np.asarray(r[k]) for r in rs], axis=0))
    r1 = _run(B, S, 1, maps)
    g1, g1s = cat(r1, "ag1_in"), cat(r1, "ag1s_in")
    r2 = _run(B, S, 2, [dict(m, ag1_out=g1, ag1s_out=g1s) for m in maps])
    g2, g2s = cat(r2, "ag2_in"), cat(r2, "ag2s_in")
    r3 = _run(B, S, 3, [dict(m, ag2_out=g2, ag2s_out=g2s) for m in maps])
    g3 = cat(r3, "ag3_in")
    r4 = _run(B, S, 4, [dict(m, ag3_out=g3, hT_imp=np.asarray(r2[c]["hT_exp"])) for c, m in enumerate(maps)])
    g4 = cat(r4, "ag4_in")
    r5 = _run(B, S, 5, [dict(m, ag4_out=g4, hT_imp=np.asarray(r4[c]["hT_exp"])) for c, m in enumerate(maps)])
    out = np.empty((B * S, D), np.float32)
    for c in range(NCORES):
        out[:, 256 * c:256 * (c + 1)] = np.asarray(r5[c]["out"]).T
    return out.reshape(B, S, D)
```
